# Optimizing a Trainium2 kernel written in Bass

```python
import math
import jax, jax.numpy as jnp
from jax import lax
import numpy as np

D_MODEL = 1024
BATCH = 8
SEQ = 4096
DEPTH = 4

HEAD_DIM = 64
EPS = 1e-6

SWA_Q_HEADS = 8
SWA_KV_HEADS = 2
SWA_GROUP = SWA_Q_HEADS // SWA_KV_HEADS
WINDOW = 128
BLOCK = 128
SWA_WIDTH = SWA_Q_HEADS * HEAD_DIM
KV_WIDTH = SWA_KV_HEADS * HEAD_DIM

SSM_WIDTH = D_MODEL // 2
SSM_GROUP_CH = 16
SSM_GROUPS = SSM_WIDTH // SSM_GROUP_CH
SSM_STATE = 64
DT_MIN = 0.001
DT_MAX = 0.1

EVEN_IN = SWA_WIDTH + 2 * KV_WIDTH + SSM_WIDTH
EVEN_MIX = SWA_WIDTH + SSM_WIDTH

SB_HEADS = D_MODEL // HEAD_DIM
SB_WIDTH = SB_HEADS * HEAD_DIM
SB_BLOCK = 128

D_FF = -(-8 * D_MODEL // (3 * 256)) * 256

N_EVEN = (DEPTH + 1) // 2
N_ODD = DEPTH // 2

kernel_name = "hybrid_swa_s5_stickbreaking_trunk"


def rmsnorm(x, g):
    xf = x.astype(jnp.float32)
    y = xf * lax.rsqrt(jnp.mean(xf * xf, axis=-1, keepdims=True) + EPS)
    return (y * g.astype(jnp.float32)).astype(x.dtype)


def swa_sink_attention(q, k, v, sinks):
    b, l, _, d = q.shape
    nb = l // BLOCK
    qb = q.reshape(b, nb, BLOCK, SWA_KV_HEADS, SWA_GROUP, d)

    def with_prev(t):
        t = t.reshape(b, nb, BLOCK, SWA_KV_HEADS, d)
        prev = jnp.concatenate([jnp.zeros_like(t[:, :1]), t[:, :-1]], axis=1)
        return jnp.concatenate([prev, t], axis=2)

    kb, vb = with_prev(k), with_prev(v)
    scores = jnp.einsum('bnqhgd,bnkhd->bnhgqk', qb, kb).astype(jnp.float32) * (d ** -0.5)
    qpos = jnp.arange(BLOCK) + BLOCK
    kpos = jnp.arange(2 * BLOCK)
    diff = qpos[:, None] - kpos[None, :]
    band = (diff >= 0) & (diff < WINDOW)
    has_prev = (jnp.arange(nb)[:, None] > 0) | (kpos[None, :] >= BLOCK)
    mask = (band[None, :, :] & has_prev[:, None, :])[None, :, None, None]
    scores = jnp.where(mask, scores, -jnp.inf)
    sink = sinks.astype(jnp.float32).reshape(SWA_KV_HEADS, SWA_GROUP)[None, None, :, :, None, None]
    m = jnp.maximum(scores.max(axis=-1, keepdims=True), sink)
    p = jnp.exp(scores - m)
    probs = p / (p.sum(axis=-1, keepdims=True) + jnp.exp(sink - m))
    o = jnp.einsum('bnhgqk,bnkhd->bnqhgd', probs.astype(v.dtype), vb)
    return o.reshape(b, l, SWA_WIDTH)


def _complex_affine_combine(left, right):
    ar1, ai1, br1, bi1 = left
    ar2, ai2, br2, bi2 = right
    return (ar2 * ar1 - ai2 * ai1,
            ar2 * ai1 + ai2 * ar1,
            ar2 * br1 - ai2 * bi1 + br2,
            ar2 * bi1 + ai2 * br1 + bi2)


def s5_mixer(u, a_re, a_im, b_re, b_im, c_re, c_im, d_skip, log_dt, w_glu, b_glu):
    f32 = jnp.float32
    b, l, _ = u.shape
    uf = u.astype(f32)
    ug = uf.reshape(b, l, SSM_GROUPS, SSM_GROUP_CH)
    a_re, a_im = a_re.astype(f32), a_im.astype(f32)
    b_re, b_im = b_re.astype(f32), b_im.astype(f32)
    c_re, c_im = c_re.astype(f32), c_im.astype(f32)
    dt = jnp.exp(log_dt.astype(f32))[:, None]
    mag = jnp.exp(a_re * dt)
    lam_re = mag * jnp.cos(a_im * dt)
    lam_im = mag * jnp.sin(a_im * dt)
    den = a_re * a_re + a_im * a_im
    w_re = ((lam_re - 1.0) * a_re + lam_im * a_im) / den
    w_im = (lam_im * a_re - (lam_re - 1.0) * a_im) / den
    bb_re = w_re[..., None] * b_re - w_im[..., None] * b_im
    bb_im = w_re[..., None] * b_im + w_im[..., None] * b_re
    bu_re = jnp.einsum('blgp,gnp->blgn', ug, bb_re)
    bu_im = jnp.einsum('blgp,gnp->blgn', ug, bb_im)
    shape = (1, l, SSM_GROUPS, SSM_STATE)
    lr = jnp.broadcast_to(lam_re, shape)
    li = jnp.broadcast_to(lam_im, shape)
    _, _, h_re, h_im = lax.associative_scan(_complex_affine_combine, (lr, li, bu_re, bu_im), axis=1)
    y = jnp.einsum('blgn,gpn->blgp', h_re, c_re) - jnp.einsum('blgn,gpn->blgp', h_im, c_im)
    y = y.reshape(b, l, SSM_WIDTH) + d_skip.astype(f32) * uf
    y = jax.nn.gelu(y)
    out = y * jax.nn.sigmoid(y @ w_glu.astype(f32) + b_glu.astype(f32))
    return out.astype(u.dtype)


def stick_breaking_attention(q, k, v):
    b, l, h, d = q.shape
    nb = l // SB_BLOCK
    qb = q.reshape(b, nb, SB_BLOCK, h, d).transpose(1, 0, 3, 2, 4)
    kt = k.transpose(0, 2, 1, 3)
    vt = v.transpose(0, 2, 1, 3)
    kpos = jnp.arange(l)
    scale = d ** -0.5

    def block(args):
        q_blk, start = args
        z = jnp.einsum('bhqd,bhkd->bhqk', q_blk, kt).astype(jnp.float32) * scale
        qpos = start + jnp.arange(SB_BLOCK)
        mask = kpos[None, :] < qpos[:, None]
        log_keep = jnp.where(mask, jax.nn.log_sigmoid(-z), 0.0)
        log_survive = lax.cumsum(log_keep, axis=3, reverse=True) - log_keep
        weights = jnp.where(mask, jnp.exp(jax.nn.log_sigmoid(z) + log_survive), 0.0)
        return jnp.einsum('bhqk,bhkd->bhqd', weights.astype(vt.dtype), vt)

    starts = jnp.arange(nb, dtype=jnp.int32) * SB_BLOCK
    o = lax.map(block, (qb, starts))
    return o.transpose(1, 0, 3, 2, 4).reshape(b, l, h * d)


def _normal(key, shape, scale):
    return jax.random.normal(key, shape, jnp.float32) * scale


def _gain(key, shape):
    return 1.0 + 0.02 * jax.random.normal(key, shape, jnp.float32)


def setup_inputs(seed: int = 0) -> dict:
    key = jax.random.key(seed)
    ks = jax.random.split(key, 26)
    E, O, L = N_EVEN, N_ODD, DEPTH
    G, N, P = SSM_GROUPS, SSM_STATE, SSM_GROUP_CH
    n = jnp.arange(N, dtype=jnp.float32)
    log_dt = jax.random.uniform(ks[13], (E, G), jnp.float32,
                                math.log(DT_MIN), math.log(DT_MAX))
    return {
        "x": _normal(ks[0], (BATCH, SEQ, D_MODEL), 1.0),
        "even_norm": _gain(ks[1], (E, D_MODEL)),
        "even_w_in": _normal(ks[2], (E, D_MODEL, EVEN_IN), D_MODEL ** -0.5),
        "q_norm": _gain(ks[3], (E, HEAD_DIM)),
        "k_norm": _gain(ks[4], (E, HEAD_DIM)),
        "sinks": _normal(ks[5], (E, SWA_Q_HEADS), 0.5),
        "ssm_a_re": -0.5 * jnp.exp(0.02 * jax.random.normal(ks[6], (E, G, N), jnp.float32)),
        "ssm_a_im": jnp.pi * n + 0.01 * jax.random.normal(ks[7], (E, G, N), jnp.float32),
        "ssm_b_re": _normal(ks[8], (E, G, N, P), (2 * P) ** -0.5),
        "ssm_b_im": _normal(ks[9], (E, G, N, P), (2 * P) ** -0.5),
        "ssm_c_re": _normal(ks[10], (E, G, P, N), N ** -0.5),
        "ssm_c_im": _normal(ks[11], (E, G, P, N), N ** -0.5),
        "ssm_d": _normal(ks[12], (E, SSM_WIDTH), 0.5),
        "ssm_log_dt": log_dt,
        "ssm_w_glu": _normal(ks[14], (E, SSM_WIDTH, SSM_WIDTH), SSM_WIDTH ** -0.5),
        "ssm_b_glu": _normal(ks[15], (E, SSM_WIDTH), 0.02),
        "even_w_out": _normal(ks[16], (E, EVEN_MIX, D_MODEL), EVEN_MIX ** -0.5),
        "odd_norm": _gain(ks[17], (O, D_MODEL)),
        "odd_w_in": _normal(ks[18], (O, D_MODEL, 3 * SB_WIDTH), D_MODEL ** -0.5),
        "odd_w_out": _normal(ks[19], (O, SB_WIDTH, D_MODEL), SB_WIDTH ** -0.5),
        "ffn_norm": _gain(ks[20], (L, D_MODEL)),
        "ffn_w_gate": _normal(ks[21], (L, D_MODEL, D_FF), D_MODEL ** -0.5),
        "ffn_w_up": _normal(ks[22], (L, D_MODEL, D_FF), D_MODEL ** -0.5),
        "ffn_w_down": _normal(ks[23], (L, D_FF, D_MODEL), D_FF ** -0.5),
    }


def reference(x, even_norm, even_w_in, q_norm, k_norm, sinks,
              ssm_a_re, ssm_a_im, ssm_b_re, ssm_b_im, ssm_c_re, ssm_c_im,
              ssm_d, ssm_log_dt, ssm_w_glu, ssm_b_glu, even_w_out,
              odd_norm, odd_w_in, odd_w_out,
              ffn_norm, ffn_w_gate, ffn_w_up, ffn_w_down):
    b, l, _ = x.shape
    for layer in range(DEPTH):
        i = layer // 2
        if layer % 2 == 0:
            hn = rmsnorm(x, even_norm[i])
            proj = hn @ even_w_in[i]
            q, k, v, u = jnp.split(
                proj, [SWA_WIDTH, SWA_WIDTH + KV_WIDTH, SWA_WIDTH + 2 * KV_WIDTH], axis=-1)
            q = rmsnorm(q.reshape(b, l, SWA_Q_HEADS, HEAD_DIM), q_norm[i])
            k = rmsnorm(k.reshape(b, l, SWA_KV_HEADS, HEAD_DIM), k_norm[i])
            v = v.reshape(b, l, SWA_KV_HEADS, HEAD_DIM)
            o_attn = swa_sink_attention(q, k, v, sinks[i])
            o_ssm = s5_mixer(u, ssm_a_re[i], ssm_a_im[i], ssm_b_re[i], ssm_b_im[i],
                             ssm_c_re[i], ssm_c_im[i], ssm_d[i], ssm_log_dt[i],
                             ssm_w_glu[i], ssm_b_glu[i])
            mixed = jnp.concatenate([o_attn, o_ssm], axis=-1) @ even_w_out[i]
        else:
            hn = rmsnorm(x, odd_norm[i])
            q, k, v = jnp.split(hn @ odd_w_in[i], 3, axis=-1)
            q = q.reshape(b, l, SB_HEADS, HEAD_DIM)
            k = k.reshape(b, l, SB_HEADS, HEAD_DIM)
            v = v.reshape(b, l, SB_HEADS, HEAD_DIM)
            mixed = stick_breaking_attention(q, k, v) @ odd_w_out[i]
        x = x + mixed
        hn = rmsnorm(x, ffn_norm[layer])
        x = x + (jax.nn.silu(hn @ ffn_w_gate[layer]) * (hn @ ffn_w_up[layer])) @ ffn_w_down[layer]
    return x
```

```python
import math
import os
from contextlib import ExitStack
import numpy as np
import concourse.bass as bass
import concourse.mybir as mybir
from concourse.bass_utils import run_bass_kernel_spmd

F32 = mybir.dt.float32
BF16 = mybir.dt.bfloat16
I32 = mybir.dt.int32
AF = mybir.ActivationFunctionType
ALU = mybir.AluOpType

EPOCH = 16000
D = 1024
DFF = 2816
NFC = DFF // 128
EPS = 1e-6
TWO_PI = 2.0 * math.pi


class Prog:
    ENG = ["pe", "act", "dve", "pool", "sp"]
    BLK = {"pe": "tensor", "act": "scalar", "dve": "vector", "pool": "gpsimd", "sp": "sync"}

    def __init__(self, nc, stack, ring=8):
        self.nc = nc
        self.stack = stack
        self.q = {e: [] for e in self.ENG}
        self.seq = {e: 0 for e in self.ENG}
        self.waited = {e: {} for e in self.ENG}
        self.lastw = {}
        self.readers = {}
        self.R = ring
        self.ring = {e: [None] * ring for e in self.ENG}
        self.ring_i = {e: 0 for e in self.ENG}
        self.dcnt = {}
        self.sems = {}
        self.ninst = 0

    def _need(self, engine, tok):
        if tok is None:
            return
        semkey, val, prod = tok
        if prod == engine and engine == "pe":
            return
        w = self.waited[engine]
        if w.get(semkey, 0) >= val:
            return
        w[semkey] = val
        self.q[engine].append(("wait", semkey, val))

    def _deps(self, engine, reads, writes):
        for r in reads:
            self._need(engine, self.lastw.get(r))
        for w in writes:
            self._need(engine, self.lastw.get(w))
            rd = self.readers.get(w)
            if rd:
                for t in rd.values():
                    self._need(engine, t)

    def _publish(self, tok, reads, writes):
        for r in reads:
            self.readers.setdefault(r, {})[tok[0]] = tok
        for w in writes:
            self.lastw[w] = tok
            self.readers[w] = {}

    def op(self, engine, fn, reads=(), writes=()):
        self._deps(engine, reads, writes)
        s = self.seq[engine]
        self.seq[engine] = s + 1
        semkey = (engine, s // EPOCH)
        tok = (semkey, s % EPOCH + 1, engine)
        self.q[engine].append(("op", fn, semkey, 1))
        self._publish(tok, reads, writes)
        return tok

    def dma(self, engine, fn, reads=(), writes=()):
        self._deps(engine, reads, writes)
        j = self.ring_i[engine] % self.R
        self.ring_i[engine] += 1
        self._need(engine, self.ring[engine][j])
        c = self.dcnt.get((engine, j), 0)
        self.dcnt[(engine, j)] = c + 1
        semkey = ("dma", engine, j, c // 900)
        v = (c % 900 + 1) * 16
        self.q[engine].append(("op", fn, semkey, 16))
        tok = (semkey, v, "dma_" + engine)
        self.ring[engine][j] = tok
        self._publish(tok, reads, writes)
        return tok

    def _last(self, e):
        s = self.seq[e]
        if s == 0:
            return None
        return ((e, (s - 1) // EPOCH), (s - 1) % EPOCH + 1, e + "_x")

    def barrier(self):
        toks = []
        for e in self.ENG:
            toks.extend(t for t in self.ring[e] if t is not None)
            t = self._last(e)
            if t is not None:
                toks.append(t)
        for e in self.ENG:
            for t in toks:
                if t[2] == e + "_x":
                    continue
                self._need(e, t)

    def emit(self):
        nc = self.nc
        for e in self.ENG:
            for it in self.q[e]:
                k = it[1] if it[0] == "wait" else it[2]
                if k not in self.sems:
                    self.sems[k] = self.stack.enter_context(nc.semaphore("s%d" % len(self.sems)))
        sems = self.sems
        with nc.Block() as block:
            for e in self.ENG:
                items = self.q[e]
                if not items:
                    continue

                def body(eng, items=items):
                    for it in items:
                        if it[0] == "wait":
                            eng.wait_ge(sems[it[1]], it[2])
                        else:
                            it[1](eng).then_inc(sems[it[2]], it[3])

                getattr(block, self.BLK[e])(body)
                self.ninst += len(items)
        self.q = {e: [] for e in self.ENG}


class Rot:
    def __init__(self, items):
        self.items = items
        self.i = 0

    def next(self):
        it = self.items[self.i % len(self.items)]
        self.i += 1
        return it


C_ID = 0
C_ONE = 128
C_TN = 256
C_UN = 384
C_SBM = 512
C_MC = 2560
C_MP = 3072
C_IOTA = 3584
C_N = 3840


def make_consts():
    c = np.zeros((128, C_N), np.float32)
    j = np.arange(128)[:, None]
    s = np.arange(128)[None, :]
    c[:, C_ID:C_ID + 128] = (j == s)
    c[:, C_ONE:C_ONE + 128] = 1.0
    c[:, C_TN:C_TN + 128] = -1.0 * (j >= s)
    c[:, C_UN:C_UN + 128] = -1.0 * (j < s)
    for r in range(4):
        for qi in range(4):
            blk = np.ones((128, 128)) if qi > r else ((j < s) * 1.0 if qi == r else np.zeros((128, 128)))
            c[:, C_SBM + r * 512 + qi * 128: C_SBM + r * 512 + (qi + 1) * 128] = blk
    for hh in range(4):
        c[:, C_MC + hh * 128: C_MC + (hh + 1) * 128] = (j <= s)
        c[:, C_MP + hh * 128: C_MP + (hh + 1) * 128] = (j > s)
    c[:, C_IOTA:C_IOTA + 256] = np.arange(1, 257, dtype=np.float32)[None, :]
    return c


def build_program(SEQ, NL):
    NT = SEQ // 512
    NTE = SEQ // 256
    NB = SEQ // 128
    NE = (NL + 1) // 2
    NO = NL // 2
    nc = bass.Bass("TRN2", target_bir_lowering=False)

    def din(name, shape, dt=F32):
        return nc.dram_tensor(name, shape, dt, kind="ExternalInput").ap()

    def dscr(name, shape, dt):
        return nc.dram_tensor(name, shape, dt, kind="Internal").ap()

    x_in = din("x", [SEQ, D])
    y_out = nc.dram_tensor("y", [SEQ, D], F32, kind="ExternalOutput").ap()
    consts = din("consts", [128, C_N])
    norms = din("norms", [128, NL * 16])
    qkg = din("qkg", [64, max(NE, 1) * 2])
    sinks = din("sinks", [1, max(NE, 1) * 8])
    ssmv = din("ssmv", [128, max(NE, 1) * 48])
    ssmb = din("ssmb", [128, max(NE, 1) * 512])
    ssmc = din("ssmc", [128, max(NE, 1) * 4096])
    ssmd = din("ssmd", [128, max(NE, 1) * 8])
    w_even_in = din("even_w_in", [max(NE, 1), D, 1280])
    w_glu = din("ssm_w_glu", [max(NE, 1), 512, 512])
    w_even_out = din("even_w_out", [max(NE, 1), D, D])
    w_odd_in = din("odd_w_in", [max(NO, 1), D, 3072])
    w_odd_out = din("odd_w_out", [max(NO, 1), D, D])
    w_gate = din("ffn_w_gate", [NL, D, DFF])
    w_up = din("ffn_w_up", [NL, D, DFF])
    w_down = din("ffn_w_down", [NL, DFF, D])
    b_even_in = dscr("b_even_in", [max(NE, 1), D, 1280], BF16)
    b_glu = dscr("b_glu", [max(NE, 1), 512, 512], BF16)
    b_even_out = dscr("b_even_out", [max(NE, 1), D, D], BF16)
    b_odd_in = dscr("b_odd_in", [max(NO, 1), D, 3072], BF16)
    b_odd_out = dscr("b_odd_out", [max(NO, 1), D, D], BF16)
    b_gate = dscr("b_gate", [NL, D, DFF], BF16)
    b_up = dscr("b_up", [NL, D, DFF], BF16)
    b_down = dscr("b_down", [NL, DFF, D], BF16)
    xs = [dscr("xs0", [D, SEQ], F32), dscr("xs1", [D, SEQ], F32)]
    mixT = dscr("mixT", [D, SEQ], BF16)
    qT = dscr("qT", [D, SEQ], BF16)
    kT = dscr("kT", [D, SEQ], BF16)
    vtm = dscr("vtm", [SEQ, D], BF16)

    with ExitStack() as glob:
        p = Prog(nc, glob)
        ps = [glob.enter_context(nc.psum_tensor("ps%d" % i, [128, 512], F32)) for i in range(8)]
        psn = ["ps%d" % i for i in range(8)]

        def conv(src, dst, rows, cols, tag):
            a = 1
            while cols // a > 2048 or cols % a:
                a += 1
            b = cols // a
            s2 = src.rearrange("r (a b) -> (r a) b", b=b)
            d2 = dst.rearrange("r (a b) -> (r a) b", b=b)
            R = rows * a
            step = 1024
            for r0 in range(0, R, step):
                r1 = min(R, r0 + step)
                p.dma("pool", lambda e, r0=r0, r1=r1: e.dma_start(out=d2[r0:r1, :], in_=s2[r0:r1, :]),
                      reads=[], writes=[tag])

        for l in range(NL):
            i = l // 2
            if l % 2 == 0:
                conv(w_even_in[i], b_even_in[i], D, 1280, "b_even_in%d" % i)
                conv(w_glu[i], b_glu[i], 512, 512, "b_glu%d" % i)
                conv(w_even_out[i], b_even_out[i], D, D, "b_even_out%d" % i)
            else:
                conv(w_odd_in[i], b_odd_in[i], D, 3072, "b_odd_in%d" % i)
                conv(w_odd_out[i], b_odd_out[i], D, D, "b_odd_out%d" % i)
            conv(w_gate[l], b_gate[l], D, DFF, "b_gate%d" % l)
            conv(w_up[l], b_up[l], D, DFF, "b_up%d" % l)
            conv(w_down[l], b_down[l], DFF, D, "b_down%d" % l)

        ucnt = [0]

        def uniq(n):
            ucnt[0] += 1
            return "%s_u%d" % (n, ucnt[0])

        def load_consts(ph, T, cols_f32=(), cols_bf16=()):
            out = {}
            for name, c0, n in cols_f32:
                t = T("cf_" + name, [128, n], F32)
                p.dma("sp", lambda e, t=t, c0=c0, n=n: e.dma_start(out=t[:], in_=consts[:, c0:c0 + n]), writes=["cf_" + name])
                out[name] = t
            for name, c0, n in cols_bf16:
                t32 = T("cs_" + name, [128, n], F32)
                t = T("cb_" + name, [128, n], BF16)
                p.dma("sp", lambda e, t32=t32, c0=c0, n=n: e.dma_start(out=t32[:], in_=consts[:, c0:c0 + n]), writes=["cs_" + name])
                p.op("dve", lambda e, t=t, t32=t32: e.tensor_copy(out=t[:], in_=t32[:]), reads=["cs_" + name], writes=["cb_" + name])
                out[name] = t
            return out

        def rms_norm(x32, xname, hn, hname, sq, rs, gt, gcol, ones16, W, bank):
            for c in range(8):
                p.op("act", lambda e, c=c: e.activation(out=sq[:, c, :], in_=x32[:, c, :], func=AF.Square),
                     reads=[xname], writes=["sq"])
            for c in range(8):
                p.op("pe", lambda e, c=c: e.matmul(ps[bank][:, 0:W], ones16[:], sq[:, c, :], start=(c == 0), stop=(c == 7)),
                     reads=["sq", "cb_ones"], writes=[psn[bank]])
            p.op("act", lambda e: e.activation(out=rs[:], in_=ps[bank][:, 0:W], func=AF.Ln, scale=1.0 / D, bias=EPS),
                 reads=[psn[bank]], writes=["rs"])
            p.op("act", lambda e: e.activation(out=rs[:], in_=rs[:], func=AF.Exp, scale=-0.5), reads=["rs"], writes=["rs"])
            for c in range(8):
                p.op("dve", lambda e, c=c: e.scalar_tensor_tensor(out=hn[:, c, :], in0=x32[:, c, :], scalar=gt[:, gcol + c:gcol + c + 1],
                                                                    in1=rs[:], op0=ALU.mult, op1=ALU.mult),
                     reads=[xname, "rs", "gt"], writes=[hname])

        def load_x_tokmajor(xtok, x32, xname, ident, t0, W, bankrot):
            nb = W // 128
            p.dma("sp", lambda e: e.dma_start(out=xtok[:, 0:nb, :], in_=x_in[t0:t0 + W, :].rearrange("(b p) f -> p b f", p=128)),
                  writes=["xtok"])
            for c in range(8):
                bank = bankrot.next()
                for b in range(nb):
                    p.op("pe", lambda e, c=c, b=b, bank=bank: e.transpose(ps[bank][:, b * 128:(b + 1) * 128], xtok[:, b, c * 128:(c + 1) * 128], ident[:]),
                         reads=["xtok", "cf_ident"], writes=[psn[bank]])
                p.op("act", lambda e, c=c, bank=bank: e.activation(out=x32[:, c, :], in_=ps[bank][:, 0:W], func=AF.Copy),
                     reads=[psn[bank]], writes=[xname])

        def phase_E(l, src, first):
            i = l // 2
            W = 256
            with ExitStack() as ph:
                def T(n, s, d):
                    return ph.enter_context(nc.sbuf_tensor(uniq(n), s, d))
                cst = load_consts(ph, T,
                                  cols_f32=[("ident", C_ID, 128), ("iota", C_IOTA, 256)],
                                  cols_bf16=[("ones", C_ONE, 128), ("mc", C_MC, 512), ("mp", C_MP, 512)])
                ident, iota, ones16, mc, mp = cst["ident"], cst["iota"], cst["ones"], cst["mc"], cst["mp"]
                gt = T("gt", [128, NL * 16], F32)
                p.dma("sp", lambda e: e.dma_start(out=gt[:], in_=norms), writes=["gt"])
                win = T("win", [128, 8, 1280], BF16)
                for c in range(8):
                    p.dma("sp", lambda e, c=c: e.dma_start(out=win[:, c, :], in_=b_even_in[i][c * 128:(c + 1) * 128, :]),
                          reads=["b_even_in%d" % i], writes=["win"])
                wgl = T("wgl", [128, 4, 512], BF16)
                p.dma("sp", lambda e: e.dma_start(out=wgl[:], in_=b_glu[i].rearrange("(c p) o -> p c o", p=128)),
                      reads=["b_glu%d" % i], writes=["wgl"])
                qk = T("qk", [64, 2], F32)
                p.dma("sp", lambda e: e.dma_start(out=qk[:], in_=qkg[:, 2 * i:2 * i + 2]), writes=["qk"])
                snk = T("snk", [1, 8], F32)
                p.dma("sp", lambda e: e.dma_start(out=snk[:], in_=sinks[:, 8 * i:8 * i + 8]), writes=["snk"])
                sv = T("sv", [128, 48], F32)
                p.dma("sp", lambda e: e.dma_start(out=sv[:], in_=ssmv[:, 48 * i:48 * i + 48]), writes=["sv"])
                sb = T("sb", [128, 2, 16, 16], F32)
                p.dma("sp", lambda e: e.dma_start(out=sb[:].rearrange("p a s q -> p (a s q)"), in_=ssmb[:, 512 * i:512 * i + 512]), writes=["sb"])
                sd = T("sd", [128, 8], F32)
                p.dma("sp", lambda e: e.dma_start(out=sd[:], in_=ssmd[:, 8 * i:8 * i + 8]), writes=["sd"])
                ct = T("ct", [128, 32, 128], BF16)
                ctmp = T("ctmp", [128, 1024], F32)
                for q4 in range(4):
                    p.dma("sp", lambda e, q4=q4: e.dma_start(out=ctmp[:], in_=ssmc[:, 4096 * i + q4 * 1024: 4096 * i + (q4 + 1) * 1024]),
                          writes=["ctmp"])
                    p.op("dve", lambda e, q4=q4: e.tensor_scalar(out=ct[:, q4 * 8:(q4 + 1) * 8, :].rearrange("p a b -> p (a b)"), in0=ctmp[:],
                                                                 scalar1=(1.0 if q4 < 2 else -1.0), scalar2=None, op0=ALU.mult),
                         reads=["ctmp"], writes=["ct"])
                zrow = T("zrow", [1, 128], F32)
                p.op("dve", lambda e: e.memset(zrow[:], 0.0), writes=["zrow"])
                esr = T("esr", [1, 1024], BF16)
                for h in range(8):
                    p.op("act", lambda e, h=h: e.activation(out=esr[0:1, h * 128:(h + 1) * 128], in_=zrow[:], func=AF.Exp, bias=snk[0:1, h:h + 1]),
                         reads=["zrow", "snk"], writes=["esr"])
                a_re, a_im, ldt = sv[:, 0:16], sv[:, 16:32], sv[:, 32:48]
                sm = T("sm", [128, 16, 16], F32)
                def V(k):
                    return sm[:, k, :]
                def dv(fn, r=("sm", "sv"), w=("sm",)):
                    p.op("dve", fn, reads=list(r), writes=list(w))
                def ac(fn, r=("sm", "sv"), w=("sm",)):
                    p.op("act", fn, reads=list(r), writes=list(w))
                kint = T("kint", [128, 256], I32)
                kflt = T("kflt", [128, 256], F32)

                def sincos(dst, src_ap, shift, n):
                    dv(lambda e: e.tensor_scalar(out=kint[:, 0:n], in0=src_ap, scalar1=shift, scalar2=1.0 / TWO_PI, op0=ALU.add, op1=ALU.mult),
                       r=("sm", "sv", "tabang", "kflt"), w=("kint",))
                    dv(lambda e: e.tensor_copy(out=kflt[:, 0:n], in_=kint[:, 0:n]), r=("kint",), w=("kflt",))
                    dv(lambda e: e.scalar_tensor_tensor(out=kflt[:, 0:n], in0=kflt[:, 0:n], scalar=-TWO_PI, in1=src_ap, op0=ALU.mult, op1=ALU.add),
                       r=("kflt", "sm", "sv", "tabang"), w=("kflt",))
                    ac(lambda e: e.activation(out=dst, in_=kflt[:, 0:n], func=AF.Sin, bias=shift_tile(shift)), r=("kflt", "shb"), w=("sm", "tab"))

                shb = T("shb", [128, 2], F32)
                p.op("dve", lambda e: e.memset(shb[:, 0:1], 0.0), writes=["shb"])
                p.op("dve", lambda e: e.memset(shb[:, 1:2], math.pi / 2), writes=["shb"])

                def shift_tile(shift):
                    return shb[:, 0:1] if shift == 0.0 else shb[:, 1:2]

                ac(lambda e: e.activation(out=V(0), in_=ldt, func=AF.Exp))
                dv(lambda e: e.tensor_tensor(out=V(1), in0=a_re, in1=V(0), op=ALU.mult))
                ac(lambda e: e.activation(out=V(2), in_=V(1), func=AF.Exp))
                dv(lambda e: e.tensor_tensor(out=V(3), in0=a_im, in1=V(0), op=ALU.mult))
                sincos(V(4), V(3), 0.0, 16)
                sincos(V(5), V(3), math.pi / 2, 16)
                dv(lambda e: e.tensor_tensor(out=V(6), in0=V(2), in1=V(5), op=ALU.mult))
                dv(lambda e: e.tensor_tensor(out=V(7), in0=V(2), in1=V(4), op=ALU.mult))
                dv(lambda e: e.tensor_tensor(out=V(8), in0=a_re, in1=a_re, op=ALU.mult))
                dv(lambda e: e.tensor_tensor(out=V(9), in0=a_im, in1=a_im, op=ALU.mult))
                dv(lambda e: e.tensor_tensor(out=V(8), in0=V(8), in1=V(9), op=ALU.add))
                dv(lambda e: e.reciprocal(out=V(8), in_=V(8)))
                dv(lambda e: e.tensor_scalar(out=V(9), in0=V(6), scalar1=-1.0, scalar2=None, op0=ALU.add))
                dv(lambda e: e.tensor_tensor(out=V(10), in0=V(9), in1=a_re, op=ALU.mult))
                dv(lambda e: e.tensor_tensor(out=V(11), in0=V(7), in1=a_im, op=ALU.mult))
                dv(lambda e: e.tensor_tensor(out=V(10), in0=V(10), in1=V(11), op=ALU.add))
                dv(lambda e: e.tensor_tensor(out=V(10), in0=V(10), in1=V(8), op=ALU.mult))
                dv(lambda e: e.tensor_tensor(out=V(11), in0=V(7), in1=a_re, op=ALU.mult))
                dv(lambda e: e.tensor_tensor(out=V(12), in0=V(9), in1=a_im, op=ALU.mult))
                dv(lambda e: e.tensor_tensor(out=V(11), in0=V(11), in1=V(12), op=ALU.subtract))
                dv(lambda e: e.tensor_tensor(out=V(11), in0=V(11), in1=V(8), op=ALU.mult))
                bb = T("bb", [128, 2, 16, 16], F32)
                bt = T("bt", [128, 16, 16], F32)
                wre_b = V(10).unsqueeze(2).broadcast_to([128, 16, 16])
                wim_b = V(11).unsqueeze(2).broadcast_to([128, 16, 16])
                dv(lambda e: e.tensor_tensor(out=bb[:, 0], in0=sb[:, 0], in1=wre_b, op=ALU.mult), r=("sm", "sb"), w=("bb",))
                dv(lambda e: e.tensor_tensor(out=bt[:], in0=sb[:, 1], in1=wim_b, op=ALU.mult), r=("sm", "sb"), w=("bt",))
                dv(lambda e: e.tensor_tensor(out=bb[:, 0], in0=bb[:, 0], in1=bt[:], op=ALU.subtract), r=("bb", "bt"), w=("bb",))
                dv(lambda e: e.tensor_tensor(out=bb[:, 1], in0=sb[:, 1], in1=wre_b, op=ALU.mult), r=("sm", "sb"), w=("bb",))
                dv(lambda e: e.tensor_tensor(out=bt[:], in0=sb[:, 0], in1=wim_b, op=ALU.mult), r=("sm", "sb", "bb"), w=("bt",))
                dv(lambda e: e.tensor_tensor(out=bb[:, 1], in0=bb[:, 1], in1=bt[:], op=ALU.add), r=("bb", "bt"), w=("bb",))
                bbt = T("bbt", [128, 32, 128], BF16)
                bpad = T("bpad", [128, 128], F32)
                for ri in range(2):
                    for st in range(16):
                        bank = st % 2
                        p.op("dve", lambda e: e.memset(bpad[:], 0.0), writes=["bpad"])
                        for hh in range(2):
                            col = ((2 * st + hh) % 8) * 16
                            p.op("dve", lambda e, ri=ri, st=st, hh=hh, col=col: e.tensor_copy(out=bpad[hh * 64:(hh + 1) * 64, col:col + 16],
                                                                                              in_=bb[hh * 64:(hh + 1) * 64, ri, st, :]),
                                 reads=["bb"], writes=["bpad"])
                        p.op("pe", lambda e, bank=bank: e.transpose(ps[bank][:, 0:128], bpad[:], ident[:]), reads=["bpad", "cf_ident"], writes=[psn[bank]])
                        p.op("act", lambda e, ri=ri, st=st, bank=bank: e.activation(out=bbt[:, ri * 16 + st, :], in_=ps[bank][:, 0:128], func=AF.Copy),
                             reads=[psn[bank]], writes=["bbt"])
                tcos = T("tcos", [128, 16, 256], F32)
                tsin = T("tsin", [128, 16, 256], F32)
                tang = T("tang", [128, 256], F32)
                for st in range(16):
                    dv(lambda e, st=st: e.tensor_scalar(out=tang[:], in0=iota[:], scalar1=sm[:, 3, st:st + 1], scalar2=None, op0=ALU.mult),
                       r=("sm", "cf_iota", "tab"), w=("tabang",))
                    sincos(tsin[:, st, :], tang[:], 0.0, 256)
                    sincos(tcos[:, st, :], tang[:], math.pi / 2, 256)
                Hre = T("Hre", [128, 16], F32)
                Him = T("Him", [128, 16], F32)
                p.op("dve", lambda e: e.memset(Hre[:], 0.0), writes=["H"])
                p.op("dve", lambda e: e.memset(Him[:], 0.0), writes=["H"])
                kring = T("kring", [64, 2, 384], BF16)
                vring = T("vring", [128, 3, 128], BF16)
                p.op("dve", lambda e: e.memset(kring[:], 0.0), writes=["kring"])
                p.op("dve", lambda e: e.memset(vring[:], 0.0), writes=["vring"])
                onesrow = T("onesrow", [1, 64], BF16)
                p.op("dve", lambda e: e.memset(onesrow[:], 1.0), writes=["onesrow"])

                xtok = T("xtok", [128, 2, D], F32) if first else None
                x32 = T("x32", [128, 8, W], F32)
                sq = T("sq", [128, 8, W], BF16)
                hn = T("hn", [128, 8, W], BF16)
                rs = T("rs", [128, W], F32)
                q32 = T("q32", [64, 10, W], F32)
                sqq = T("sqq", [64, 10, W], BF16)
                rsq = T("rsq", [64, 512], F32)
                qn = T("qn", [64, 8, W], BF16)
                pe32 = T("pe32", [128, 512], F32)
                pT = T("pT", [128, 512], BF16)
                rec = T("rec", [64, 512], F32)
                mixa = T("mixa", [64, 8, W], BF16)
                mixs = T("mixs", [128, 4, W], BF16)
                u32 = T("u32", [128, 4, W], F32)
                u16 = T("u16", [128, 4, W], BF16)
                t1 = T("t1", [128, W], F32)
                t2 = T("t2", [128, W], F32)
                bpr = T("bpr", [128, W], F32)
                bpi = T("bpi", [128, W], F32)
                gre = T("gre", [128, W], F32)
                gim = T("gim", [128, W], F32)
                hre = T("hre", [128, W], BF16)
                him = T("him", [128, W], BF16)
                tl = T("tl", [128, 4], F32)
                y32 = T("y32", [128, 1, W], F32)
                yg32 = T("yg32", [128, 4, W], F32)
                yg16 = T("yg16", [128, 4, W], BF16)
                g1 = T("g1", [128, W], F32)
                g2 = T("g2", [128, W], F32)
                bank_rot = Rot([2, 3, 4, 5])

                for it in range(NTE):
                    t0 = it * W
                    if first:
                        load_x_tokmajor(xtok, x32, "x32", ident, t0, W, bank_rot)
                        p.dma("sp", lambda e, t0=t0: e.dma_start(out=src.rearrange("(c p) t -> p c t", p=128)[:, :, t0:t0 + W], in_=x32[:]),
                              reads=["x32"], writes=["xs_e"])
                    else:
                        p.dma("sp", lambda e, t0=t0: e.dma_start(out=x32[:], in_=src.rearrange("(c p) t -> p c t", p=128)[:, :, t0:t0 + W]),
                              reads=["xs_prev"], writes=["x32"])
                    rms_norm(x32, "x32", hn, "hn", sq, rs, gt, (2 * l) * 8, ones16, W, 0)
                    for h in range(10):
                        bank = bank_rot.next()
                        col = h * 64 if h < 8 else 512 + (h - 8) * 64
                        for c in range(8):
                            p.op("pe", lambda e, c=c, col=col, bank=bank: e.matmul(ps[bank][0:64, 0:W], win[:, c, col:col + 64], hn[:, c, :],
                                                                                    start=(c == 0), stop=(c == 7)),
                                 reads=["win", "hn"], writes=[psn[bank]])
                        p.op("act", lambda e, h=h, bank=bank: e.activation(out=q32[:, h, :], in_=ps[bank][0:64, 0:W], func=AF.Copy),
                             reads=[psn[bank]], writes=["q32"])
                        p.op("act", lambda e, h=h, bank=bank: e.activation(out=sqq[:, h, :], in_=ps[bank][0:64, 0:W], func=AF.Square),
                             reads=[psn[bank]], writes=["sqq"])
                    for j in range(5):
                        bank = bank_rot.next()
                        p.op("pe", lambda e, j=j, bank=bank: e.matmul(ps[bank][0:64, 0:512], ones16[0:64, 0:64],
                                                                       sqq[:, 2 * j:2 * j + 2, :].rearrange("p a b -> p (a b)"), start=True, stop=True),
                             reads=["sqq", "cb_ones"], writes=[psn[bank]])
                        p.op("act", lambda e, bank=bank: e.activation(out=rsq[:], in_=ps[bank][0:64, 0:512], func=AF.Ln, scale=1.0 / 64, bias=EPS),
                             reads=[psn[bank]], writes=["rsq"])
                        p.op("act", lambda e: e.activation(out=rsq[:], in_=rsq[:], func=AF.Exp, scale=-0.5), reads=["rsq"], writes=["rsq"])
                        for a in range(2):
                            h = 2 * j + a
                            if h < 8:
                                p.op("dve", lambda e, h=h, a=a: e.scalar_tensor_tensor(out=qn[:, h, :], in0=q32[:, h, :], scalar=qk[:, 0:1],
                                                                                        in1=rsq[:, a * W:(a + 1) * W], op0=ALU.mult, op1=ALU.mult),
                                     reads=["q32", "rsq", "qk"], writes=["qn"])
                            else:
                                g = h - 8
                                p.op("dve", lambda e, h=h, a=a, g=g: e.scalar_tensor_tensor(out=kring[:, g, 128:384], in0=q32[:, h, :], scalar=qk[:, 1:2],
                                                                                             in1=rsq[:, a * W:(a + 1) * W], op0=ALU.mult, op1=ALU.mult),
                                     reads=["q32", "rsq", "qk"], writes=["kring"])
                    for blk in range(2):
                        bank = bank_rot.next()
                        for c in range(8):
                            p.op("pe", lambda e, c=c, blk=blk, bank=bank: e.matmul(ps[bank][:, 0:128], hn[:, c, blk * 128:(blk + 1) * 128], win[:, c, 640:768],
                                                                                    start=(c == 0), stop=(c == 7)),
                                 reads=["win", "hn"], writes=[psn[bank]])
                        p.op("act", lambda e, blk=blk, bank=bank: e.activation(out=vring[:, 1 + blk, :], in_=ps[bank][:, 0:128], func=AF.Copy),
                             reads=[psn[bank]], writes=["vring"])
                    for g in range(2):
                        for blk in range(2):
                            gb = 2 * it + blk
                            kbs = ([(blk, mp)] if gb > 0 else []) + [(blk + 1, mc)]
                            bo, bd = 6, 7
                            for ki, (slot, msk) in enumerate(kbs):
                                bank = bank_rot.next()
                                p.op("pe", lambda e, g=g, blk=blk, slot=slot, bank=bank: e.matmul(
                                    ps[bank][:, 0:512], kring[:, g, slot * 128:(slot + 1) * 128], qn[:, 4 * g:4 * g + 4, blk * 128:(blk + 1) * 128],
                                    start=True, stop=True), reads=["kring", "qn"], writes=[psn[bank]])
                                p.op("act", lambda e, bank=bank: e.activation(out=pe32[:], in_=ps[bank][:, 0:512], func=AF.Exp, scale=0.125),
                                     reads=[psn[bank]], writes=["pe32"])
                                p.op("dve", lambda e, msk=msk: e.tensor_tensor(out=pT[:], in0=pe32[:], in1=msk[:], op=ALU.mult),
                                     reads=["pe32", "cb_mc", "cb_mp"], writes=["pT"])
                                p.op("pe", lambda e, g=g, slot=slot, ki=ki, nkb=len(kbs): e.matmul(ps[bo][0:64, 0:512], vring[:, slot, g * 64:(g + 1) * 64], pT[:],
                                                                                      start=(ki == 0), stop=(ki == nkb - 1)),
                                     reads=["vring", "pT"], writes=[psn[bo]])
                                p.op("pe", lambda e, ki=ki: e.matmul(ps[bd][0:64, 0:512], ones16[:, 0:64], pT[:], start=(ki == 0), stop=False),
                                     reads=["cb_ones", "pT"], writes=[psn[bd]])
                            p.op("pe", lambda e, g=g: e.matmul(ps[bd][0:64, 0:512], onesrow[0:1, 0:64], esr[0:1, 4 * g * 128:(4 * g + 4) * 128],
                                                               start=False, stop=True), reads=["onesrow", "esr"], writes=[psn[bd]])
                            p.op("dve", lambda e: e.reciprocal(out=rec[:], in_=ps[bd][0:64, 0:512]), reads=[psn[bd]], writes=["rec"])
                            p.op("dve", lambda e, g=g, blk=blk: e.tensor_tensor(out=mixa[:, 4 * g:4 * g + 4, blk * 128:(blk + 1) * 128],
                                                                                 in0=ps[bo][0:64, 0:512].rearrange("p (a b) -> p a b", b=128),
                                                                                 in1=rec[:].rearrange("p (a b) -> p a b", b=128), op=ALU.mult),
                                 reads=[psn[bo], "rec"], writes=["mixa"])
                    p.op("dve", lambda e: e.tensor_copy(out=kring[:, :, 0:128], in_=kring[:, :, 256:384]), reads=["kring"], writes=["kring"])
                    p.op("dve", lambda e: e.tensor_copy(out=vring[:, 0, :], in_=vring[:, 2, :]), reads=["vring"], writes=["vring"])
                    for cc in range(4):
                        bank = bank_rot.next()
                        for c in range(8):
                            p.op("pe", lambda e, c=c, cc=cc, bank=bank: e.matmul(ps[bank][:, 0:W], win[:, c, 768 + cc * 128:768 + (cc + 1) * 128], hn[:, c, :],
                                                                                  start=(c == 0), stop=(c == 7)),
                                 reads=["win", "hn"], writes=[psn[bank]])
                        p.op("act", lambda e, cc=cc, bank=bank: e.activation(out=u32[:, cc, :], in_=ps[bank][:, 0:W], func=AF.Copy),
                             reads=[psn[bank]], writes=["u32"])
                        p.op("act", lambda e, cc=cc, bank=bank: e.activation(out=u16[:, cc, :], in_=ps[bank][:, 0:W], func=AF.Copy),
                             reads=[psn[bank]], writes=["u16"])
                    for cc in range(4):
                        by = 6 + (cc % 2)
                        for si in range(4):
                            st = 4 * cc + si
                            bank = bank_rot.next()
                            p.op("pe", lambda e, st=st, cc=cc, bank=bank: e.matmul(ps[bank][:, 0:W], bbt[:, st, :], u16[:, cc, :], start=True, stop=True),
                                 reads=["bbt", "u16"], writes=[psn[bank]])
                            p.op("pe", lambda e, st=st, cc=cc, bank=bank: e.matmul(ps[bank][:, W:2 * W], bbt[:, 16 + st, :], u16[:, cc, :], start=True, stop=True),
                                 reads=["bbt", "u16"], writes=[psn[bank]])
                            bre = ps[bank][:, 0:W]
                            bim = ps[bank][:, W:2 * W]
                            cs, sn = tcos[:, st, :], tsin[:, st, :]
                            R = [psn[bank], "tab"]
                            p.op("dve", lambda e, bre=bre, cs=cs: e.tensor_tensor(out=t1[:], in0=bre, in1=cs, op=ALU.mult), reads=R, writes=["t1"])
                            p.op("dve", lambda e, bim=bim, sn=sn: e.tensor_tensor(out=t2[:], in0=bim, in1=sn, op=ALU.mult), reads=R, writes=["t2"])
                            p.op("dve", lambda e: e.tensor_tensor(out=bpr[:], in0=t1[:], in1=t2[:], op=ALU.add), reads=["t1", "t2"], writes=["bpr"])
                            p.op("dve", lambda e, bim=bim, cs=cs: e.tensor_tensor(out=t1[:], in0=bim, in1=cs, op=ALU.mult), reads=R + ["bpr"], writes=["t1"])
                            p.op("dve", lambda e, bre=bre, sn=sn: e.tensor_tensor(out=t2[:], in0=bre, in1=sn, op=ALU.mult), reads=R + ["bpr"], writes=["t2"])
                            p.op("dve", lambda e: e.tensor_tensor(out=bpi[:], in0=t1[:], in1=t2[:], op=ALU.subtract), reads=["t1", "t2"], writes=["bpi"])
                            magb = sm[:, 2, st:st + 1].broadcast_to([128, W])
                            p.op("dve", lambda e, st=st, magb=magb: e.tensor_tensor_scan(out=gre[:], data0=magb, data1=bpr[:], initial=Hre[:, st:st + 1],
                                                                                          op0=ALU.mult, op1=ALU.add),
                                 reads=["sm", "bpr", "H"], writes=["gre"])
                            p.op("dve", lambda e, st=st, magb=magb: e.tensor_tensor_scan(out=gim[:], data0=magb, data1=bpi[:], initial=Him[:, st:st + 1],
                                                                                          op0=ALU.mult, op1=ALU.add),
                                 reads=["sm", "bpi", "H"], writes=["gim"])
                            cl, sl = tcos[:, st, W - 1:W], tsin[:, st, W - 1:W]
                            grl, gil = gre[:, W - 1:W], gim[:, W - 1:W]
                            p.op("dve", lambda e, cl=cl, grl=grl: e.tensor_tensor(out=tl[:, 0:1], in0=cl, in1=grl, op=ALU.mult), reads=["gre", "tab"], writes=["tl"])
                            p.op("dve", lambda e, sl=sl, gil=gil: e.tensor_tensor(out=tl[:, 1:2], in0=sl, in1=gil, op=ALU.mult), reads=["gim", "tab"], writes=["tl"])
                            p.op("dve", lambda e, cl=cl, gil=gil: e.tensor_tensor(out=tl[:, 2:3], in0=cl, in1=gil, op=ALU.mult), reads=["gim", "tab"], writes=["tl"])
                            p.op("dve", lambda e, sl=sl, grl=grl: e.tensor_tensor(out=tl[:, 3:4], in0=sl, in1=grl, op=ALU.mult), reads=["gre", "tab"], writes=["tl"])
                            p.op("dve", lambda e, st=st: e.tensor_tensor(out=Hre[:, st:st + 1], in0=tl[:, 0:1], in1=tl[:, 1:2], op=ALU.subtract), reads=["tl"], writes=["H"])
                            p.op("dve", lambda e, st=st: e.tensor_tensor(out=Him[:, st:st + 1], in0=tl[:, 2:3], in1=tl[:, 3:4], op=ALU.add), reads=["tl"], writes=["H"])
                            p.op("dve", lambda e, cs=cs: e.tensor_tensor(out=t1[:], in0=cs, in1=gre[:], op=ALU.mult), reads=["gre", "tab"], writes=["t1"])
                            p.op("dve", lambda e, sn=sn: e.tensor_tensor(out=t2[:], in0=sn, in1=gim[:], op=ALU.mult), reads=["gim", "tab"], writes=["t2"])
                            p.op("dve", lambda e: e.tensor_tensor(out=hre[:], in0=t1[:], in1=t2[:], op=ALU.subtract), reads=["t1", "t2"], writes=["hre"])
                            p.op("dve", lambda e, cs=cs: e.tensor_tensor(out=t1[:], in0=cs, in1=gim[:], op=ALU.mult), reads=["gim", "tab", "hre"], writes=["t1"])
                            p.op("dve", lambda e, sn=sn: e.tensor_tensor(out=t2[:], in0=sn, in1=gre[:], op=ALU.mult), reads=["gre", "tab", "hre"], writes=["t2"])
                            p.op("dve", lambda e: e.tensor_tensor(out=him[:], in0=t1[:], in1=t2[:], op=ALU.add), reads=["t1", "t2"], writes=["him"])
                            p.op("pe", lambda e, st=st, si=si, by=by: e.matmul(ps[by][:, 0:W], ct[:, st, :], hre[:], start=(si == 0), stop=False),
                                 reads=["ct", "hre"], writes=[psn[by]])
                            p.op("pe", lambda e, st=st, si=si, by=by: e.matmul(ps[by][:, 0:W], ct[:, 16 + st, :], him[:], start=False, stop=(si == 3)),
                                 reads=["ct", "him"], writes=[psn[by]])
                        p.op("dve", lambda e, cc=cc, by=by: e.scalar_tensor_tensor(out=y32[:, 0, :], in0=u32[:, cc, :], scalar=sd[:, cc:cc + 1], in1=ps[by][:, 0:W],
                                                                                   op0=ALU.mult, op1=ALU.add),
                             reads=["u32", "sd", psn[by]], writes=["y32"])
                        p.op("act", lambda e, cc=cc: e.activation(out=g1[:], in_=y32[:, 0, :], func=AF.Square), reads=["y32"], writes=["g1"])
                        p.op("dve", lambda e: e.tensor_scalar(out=g1[:], in0=g1[:], scalar1=0.044715, scalar2=1.0, op0=ALU.mult, op1=ALU.add),
                             reads=["g1"], writes=["g1"])
                        p.op("dve", lambda e, cc=cc: e.tensor_tensor(out=g1[:], in0=g1[:], in1=y32[:, 0, :], op=ALU.mult), reads=["g1", "y32"], writes=["g1"])
                        p.op("act", lambda e: e.activation(out=g2[:], in_=g1[:], func=AF.Sigmoid, scale=2.0 * math.sqrt(2.0 / math.pi)),
                             reads=["g1"], writes=["g2"])
                        p.op("dve", lambda e, cc=cc: e.tensor_tensor(out=yg32[:, cc, :], in0=y32[:, 0, :], in1=g2[:], op=ALU.mult),
                             reads=["y32", "g2"], writes=["yg32"])
                        p.op("act", lambda e, cc=cc: e.activation(out=yg16[:, cc, :], in_=yg32[:, cc, :], func=AF.Copy), reads=["yg32"], writes=["yg16"])
                    for o in range(4):
                        bank = bank_rot.next()
                        for cc in range(4):
                            p.op("pe", lambda e, o=o, cc=cc, bank=bank: e.matmul(ps[bank][:, 0:W], wgl[:, cc, o * 128:(o + 1) * 128], yg16[:, cc, :],
                                                                                  start=(cc == 0), stop=(cc == 3)),
                                 reads=["wgl", "yg16"], writes=[psn[bank]])
                        p.op("act", lambda e, o=o, bank=bank: e.activation(out=g2[:], in_=ps[bank][:, 0:W], func=AF.Sigmoid, bias=sd[:, 4 + o:5 + o]),
                             reads=[psn[bank], "sd"], writes=["g2"])
                        p.op("dve", lambda e, o=o: e.tensor_tensor(out=mixs[:, o, :], in0=yg32[:, o, :], in1=g2[:], op=ALU.mult),
                             reads=["yg32", "g2"], writes=["mixs"])
                    p.dma("sp", lambda e, t0=t0: e.dma_start(out=mixT[0:512, :].rearrange("(h p) t -> p h t", p=64)[:, :, t0:t0 + W], in_=mixa[:]),
                          reads=["mixa"], writes=["mixT"])
                    p.dma("sp", lambda e, t0=t0: e.dma_start(out=mixT[512:1024, :].rearrange("(c p) t -> p c t", p=128)[:, :, t0:t0 + W], in_=mixs[:]),
                          reads=["mixs"], writes=["mixT"])
                p.barrier()
                p.emit()

        def phase_A(l, src):
            i = l // 2
            W = 512
            with ExitStack() as ph:
                def T(n, s, d):
                    return ph.enter_context(nc.sbuf_tensor(uniq(n), s, d))
                cst = load_consts(ph, T, cols_bf16=[("ones", C_ONE, 128)])
                ones16 = cst["ones"]
                gt = T("gt", [128, NL * 16], F32)
                p.dma("sp", lambda e: e.dma_start(out=gt[:], in_=norms), writes=["gt"])
                win = T("win", [128, 8, 3072], BF16)
                for c in range(8):
                    p.dma("sp", lambda e, c=c: e.dma_start(out=win[:, c, :], in_=b_odd_in[i][c * 128:(c + 1) * 128, :]),
                          reads=["b_odd_in%d" % i], writes=["win"])
                x32s = [T("x32_%d" % k, [128, 8, W], F32) for k in range(2)]
                sq = T("sq", [128, 8, W], BF16)
                hn = T("hn", [128, 8, W], BF16)
                rs = T("rs", [128, W], F32)
                o16 = [T("o16_%d" % k, [128, W], BF16) for k in range(4)]
                v16 = [T("v16_%d" % k, [128, D], BF16) for k in range(2)]
                bank_rot = Rot([1, 2, 3, 4, 5, 6, 7])
                for it in range(NT):
                    t0 = it * W
                    x32 = x32s[it % 2]
                    xn = "x32_%d" % (it % 2)
                    p.dma("sp", lambda e, t0=t0, x32=x32: e.dma_start(out=x32[:], in_=src.rearrange("(c p) t -> p c t", p=128)[:, :, t0:t0 + W]),
                          reads=["xs_prev"], writes=[xn])
                    rms_norm(x32, xn, hn, "hn", sq, rs, gt, (2 * l) * 8, ones16, W, 0)
                    for j in range(16):
                        bank = bank_rot.next()
                        ob = o16[j % 4]
                        on = "o16_%d" % (j % 4)
                        for c in range(8):
                            p.op("pe", lambda e, c=c, j=j, bank=bank: e.matmul(ps[bank][:, 0:W], win[:, c, j * 128:(j + 1) * 128], hn[:, c, :],
                                                                                start=(c == 0), stop=(c == 7)),
                                 reads=["win", "hn"], writes=[psn[bank]])
                        p.op("act", lambda e, ob=ob, bank=bank: e.activation(out=ob[:], in_=ps[bank][:, 0:W], func=AF.Copy), reads=[psn[bank]], writes=[on])
                        dst = qT if j < 8 else kT
                        r0 = (j % 8) * 128
                        p.dma("sp", lambda e, dst=dst, r0=r0, t0=t0, ob=ob: e.dma_start(out=dst[r0:r0 + 128, t0:t0 + W], in_=ob[:]),
                              reads=[on], writes=["qkT"])
                    for b in range(4):
                        vb = v16[b % 2]
                        vn = "v16_%d" % (b % 2)
                        for half in range(2):
                            bank = bank_rot.next()
                            for c in range(8):
                                p.op("pe", lambda e, c=c, b=b, half=half, bank=bank: e.matmul(ps[bank][:, 0:512], hn[:, c, b * 128:(b + 1) * 128],
                                                                                              win[:, c, 2048 + half * 512:2048 + (half + 1) * 512],
                                                                                              start=(c == 0), stop=(c == 7)),
                                     reads=["win", "hn"], writes=[psn[bank]])
                            p.op("act", lambda e, vb=vb, half=half, bank=bank: e.activation(out=vb[:, half * 512:(half + 1) * 512], in_=ps[bank][:, 0:512], func=AF.Copy),
                                 reads=[psn[bank]], writes=[vn])
                        p.dma("sp", lambda e, vb=vb, b=b, t0=t0: e.dma_start(out=vtm[t0 + b * 128:t0 + (b + 1) * 128, :], in_=vb[:]),
                              reads=[vn], writes=["vtm"])
                p.barrier()
                p.emit()

        def phase_B(l):
            W = 512
            with ExitStack() as ph:
                def T(n, s, d):
                    return ph.enter_context(nc.sbuf_tensor(uniq(n), s, d))
                cst = load_consts(ph, T, cols_bf16=[("tn", C_TN, 128), ("un", C_UN, 128), ("sbm", C_SBM, 2048)])
                tn, un, sbm = cst["tn"], cst["un"], cst["sbm"]
                kTp = [T("kTp%d" % k, [128, SEQ], BF16) for k in range(2)]
                qTp = [T("qTp%d" % k, [128, SEQ], BF16) for k in range(2)]
                vp = [T("vp%d" % k, [128, NB, 128], BF16) for k in range(2)]
                e32 = [[T("e32_%d_%d" % (h, k), [128, W], F32) for k in range(2)] for h in range(2)]
                sp16 = [[T("sp16_%d_%d" % (h, k), [128, W], BF16) for k in range(2)] for h in range(2)]
                ea = [T("ea_%d" % h, [128, W], F32) for h in range(2)]
                w16 = [[T("w16_%d_%d" % (h, k), [128, W], BF16) for k in range(2)] for h in range(2)]
                oo = [T("oo_%d" % k, [128, W], BF16) for k in range(2)]
                for hp in range(8):
                    k = hp % 2
                    kt, qt, vt = kTp[k], qTp[k], vp[k]
                    p.dma("sp", lambda e, kt=kt, hp=hp: e.dma_start(out=kt[:], in_=kT[hp * 128:(hp + 1) * 128, :]), reads=["qkT"], writes=["kTp%d" % k])
                    p.dma("sp", lambda e, qt=qt, hp=hp: e.dma_start(out=qt[:], in_=qT[hp * 128:(hp + 1) * 128, :]), reads=["qkT"], writes=["qTp%d" % k])
                    for b0 in range(0, NB, 8):
                        b1 = min(NB, b0 + 8)
                        p.dma("sp", lambda e, vt=vt, hp=hp, b0=b0, b1=b1: e.dma_start(
                            out=vt[:, b0:b1, :], in_=vtm[b0 * 128:b1 * 128, hp * 128:(hp + 1) * 128].rearrange("(b p) f -> p b f", p=128)),
                            reads=["vtm"], writes=["vp%d" % k])
                    RK = ["kTp%d" % k, "qTp%d" % k]
                    for it in range(NT):
                        t0 = it * W
                        steps = list(range(4 * it + 3, -1, -1))
                        n = len(steps)

                        def stage1(idx, kb, kt=kt, qt=qt, t0=t0, it=it):
                            par = idx % 2
                            diag = kb >= 4 * it
                            for h in range(2):
                                zb = h * 2 + par
                                hs = slice(h * 64, (h + 1) * 64)
                                p.op("pe", lambda e, zb=zb, hs=hs, kb=kb, kt=kt, qt=qt, t0=t0: e.matmul(
                                    ps[zb][:, 0:W], kt[hs, kb * 128:(kb + 1) * 128], qt[hs, t0:t0 + W], start=True, stop=True),
                                    reads=RK, writes=[psn[zb]])
                                en = "e32_%d_%d" % (h, par)
                                sn_ = "sp16_%d_%d" % (h, par)
                                et, st_ = e32[h][par], sp16[h][par]
                                p.op("act", lambda e, et=et, zb=zb: e.activation(out=et[:], in_=ps[zb][:, 0:W], func=AF.Exp, scale=0.125),
                                     reads=[psn[zb]], writes=[en])
                                p.op("act", lambda e, et=et, st_=st_: e.activation(out=st_[:], in_=et[:], func=AF.Ln, bias=1.0), reads=[en], writes=[sn_])
                                if diag:
                                    m = sbm[:, (kb - 4 * it) * 512:(kb - 4 * it + 1) * 512]
                                    p.op("pool", lambda e, st_=st_, m=m: e.tensor_tensor(out=st_[:], in0=st_[:], in1=m, op=ALU.mult),
                                         reads=[sn_, "cb_sbm"], writes=[sn_])
                                    p.op("pool", lambda e, et=et, m=m: e.tensor_tensor(out=et[:], in0=et[:], in1=m, op=ALU.mult),
                                         reads=[en, "cb_sbm"], writes=[en])

                        def stage2(idx, kb, vt=vt, n=n, k=k):
                            par = idx % 2
                            for h in range(2):
                                ab, ob = 4 + h, 6 + h
                                hs = slice(h * 64, (h + 1) * 64)
                                en = "e32_%d_%d" % (h, par)
                                sn_ = "sp16_%d_%d" % (h, par)
                                wn = "w16_%d_%d" % (h, par)
                                et, st_, wt = e32[h][par], sp16[h][par], w16[h][par]
                                p.op("pe", lambda e, ab=ab, st_=st_, idx=idx, n=n: e.matmul(ps[ab][:, 0:W], tn[:], st_[:], start=(idx == 0), stop=(idx == n - 1)),
                                     reads=["cb_tn", sn_], writes=[psn[ab]])
                                p.op("act", lambda e, ab=ab, h=h: e.activation(out=ea[h][:], in_=ps[ab][:, 0:W], func=AF.Exp),
                                     reads=[psn[ab]], writes=["ea_%d" % h])
                                p.op("dve", lambda e, wt=wt, et=et, h=h: e.tensor_tensor(out=wt[:], in0=et[:], in1=ea[h][:], op=ALU.mult),
                                     reads=[en, "ea_%d" % h], writes=[wn])
                                if idx < n - 1:
                                    p.op("pe", lambda e, ab=ab, st_=st_: e.matmul(ps[ab][:, 0:W], un[:], st_[:], start=False, stop=False),
                                         reads=["cb_un", sn_], writes=[psn[ab]])
                                p.op("pe", lambda e, ob=ob, hs=hs, kb=kb, wt=wt, vt=vt, idx=idx, n=n: e.matmul(
                                    ps[ob][hs, 0:W], vt[:, kb, hs], wt[:], start=(idx == 0), stop=(idx == n - 1)),
                                    reads=["vp%d" % k, wn], writes=[psn[ob]])

                        for idx in range(n + 1):
                            if idx < n:
                                stage1(idx, steps[idx])
                            if idx > 0:
                                stage2(idx - 1, steps[idx - 1])
                        ot = oo[it % 2]
                        on = "oo_%d" % (it % 2)
                        for h in range(2):
                            hs = slice(h * 64, (h + 1) * 64)
                            p.op("act", lambda e, ot=ot, hs=hs, h=h: e.activation(out=ot[hs, :], in_=ps[6 + h][hs, 0:W], func=AF.Copy),
                                 reads=[psn[6 + h]], writes=[on])
                        p.dma("sp", lambda e, ot=ot, hp=hp, t0=t0: e.dma_start(out=mixT[hp * 128:(hp + 1) * 128, t0:t0 + W], in_=ot[:]),
                              reads=[on], writes=["mixT"])
                p.barrier()
                p.emit()

        def phase_F(l, src, dst, last):
            i = l // 2
            W = 512
            bo = b_even_out[i] if l % 2 == 0 else b_odd_out[i]
            bon = ("b_even_out%d" if l % 2 == 0 else "b_odd_out%d") % i
            pieces = [(f0, min(4, NFC - f0)) for f0 in range(0, NFC, 4)]
            with ExitStack() as ph:
                def T(n, s, d):
                    return ph.enter_context(nc.sbuf_tensor(uniq(n), s, d))
                cst = load_consts(ph, T, cols_f32=([("ident", C_ID, 128)] if last else []), cols_bf16=[("ones", C_ONE, 128)])
                ones16 = cst["ones"]
                ident = cst.get("ident")
                gt = T("gt", [128, NL * 16], F32)
                p.dma("sp", lambda e: e.dma_start(out=gt[:], in_=norms), writes=["gt"])
                wo = T("wo", [128, 8, D], BF16)
                for c in range(8):
                    p.dma("sp", lambda e, c=c: e.dma_start(out=wo[:, c, :], in_=bo[c * 128:(c + 1) * 128, :]), reads=[bon], writes=["wo"])
                x32s = [T("x32_%d" % k, [128, 8, W], F32) for k in range(2)]
                mixs_ = [T("mix_%d" % k, [128, 8, W], BF16) for k in range(2)]
                sq = T("sq", [128, 8, W], BF16)
                hn = T("hn", [128, 8, W], BF16)
                rs = T("rs", [128, W], F32)
                hh = T("hh", [128, NFC, W], BF16)
                wg = [T("wg_%d" % k, [128, 8, 512], BF16) for k in range(2)]
                wu = [T("wu_%d" % k, [128, 8, 512], BF16) for k in range(2)]
                wd = [T("wd_%d" % k, [128, NFC, 256], BF16) for k in range(2)]
                sg = [T("sg_%d" % k, [128, W], F32) for k in range(2)]
                ytile = T("ytile", [128, 4, D], F32) if last else None
                wcnt = [0, 0]
                for it in range(NT):
                    t0 = it * W
                    x32 = x32s[it % 2]
                    xn = "x32_%d" % (it % 2)
                    mx = mixs_[it % 2]
                    mn = "mix_%d" % (it % 2)
                    p.dma("sp", lambda e, t0=t0, x32=x32: e.dma_start(out=x32[:], in_=src.rearrange("(c p) t -> p c t", p=128)[:, :, t0:t0 + W]),
                          reads=["xs_prev", "xs_e"], writes=[xn])
                    p.dma("sp", lambda e, t0=t0, mx=mx: e.dma_start(out=mx[:], in_=mixT.rearrange("(c p) t -> p c t", p=128)[:, :, t0:t0 + W]),
                          reads=["mixT"], writes=[mn])
                    for o in range(8):
                        bank = 1 + (o % 2)
                        for c in range(8):
                            p.op("pe", lambda e, c=c, o=o, bank=bank, mx=mx: e.matmul(ps[bank][:, 0:W], wo[:, c, o * 128:(o + 1) * 128], mx[:, c, :],
                                                                                       start=(c == 0), stop=(c == 7)),
                                 reads=["wo", mn], writes=[psn[bank]])
                        p.op("dve", lambda e, o=o, bank=bank, x32=x32: e.tensor_tensor(out=x32[:, o, :], in0=x32[:, o, :], in1=ps[bank][:, 0:W], op=ALU.add),
                             reads=[xn, psn[bank]], writes=[xn])
                    rms_norm(x32, xn, hn, "hn", sq, rs, gt, (2 * l + 1) * 8, ones16, W, 0)
                    for (f0, nf) in pieces:
                        kb = wcnt[0] % 2
                        wcnt[0] += 1
                        wgt, wut = wg[kb], wu[kb]
                        for c in range(8):
                            p.dma("sp", lambda e, c=c, f0=f0, nf=nf, wgt=wgt: e.dma_start(out=wgt[:, c, 0:nf * 128],
                                                                                          in_=b_gate[l][c * 128:(c + 1) * 128, f0 * 128:(f0 + nf) * 128]),
                                  reads=["b_gate%d" % l], writes=["wg_%d" % kb])
                            p.dma("sp", lambda e, c=c, f0=f0, nf=nf, wut=wut: e.dma_start(out=wut[:, c, 0:nf * 128],
                                                                                          in_=b_up[l][c * 128:(c + 1) * 128, f0 * 128:(f0 + nf) * 128]),
                                  reads=["b_up%d" % l], writes=["wu_%d" % kb])
                        for fi in range(nf):
                            f = f0 + fi
                            bg, bu = 3 + 2 * (f % 2), 4 + 2 * (f % 2)
                            sgt = sg[f % 2]
                            for c in range(8):
                                p.op("pe", lambda e, c=c, fi=fi, bg=bg, wgt=wgt: e.matmul(ps[bg][:, 0:W], wgt[:, c, fi * 128:(fi + 1) * 128], hn[:, c, :],
                                                                                           start=(c == 0), stop=(c == 7)),
                                     reads=["wg_%d" % kb, "hn"], writes=[psn[bg]])
                            for c in range(8):
                                p.op("pe", lambda e, c=c, fi=fi, bu=bu, wut=wut: e.matmul(ps[bu][:, 0:W], wut[:, c, fi * 128:(fi + 1) * 128], hn[:, c, :],
                                                                                           start=(c == 0), stop=(c == 7)),
                                     reads=["wu_%d" % kb, "hn"], writes=[psn[bu]])
                            p.op("act", lambda e, bg=bg, sgt=sgt: e.activation(out=sgt[:], in_=ps[bg][:, 0:W], func=AF.Silu),
                                 reads=[psn[bg]], writes=["sg_%d" % (f % 2)])
                            p.op("dve", lambda e, f=f, bu=bu, sgt=sgt: e.tensor_tensor(out=hh[:, f, :], in0=sgt[:], in1=ps[bu][:, 0:W], op=ALU.mult),
                                 reads=["sg_%d" % (f % 2), psn[bu]], writes=["hh"])
                    for op_ in range(4):
                        kb = wcnt[1] % 2
                        wcnt[1] += 1
                        wdt = wd[kb]
                        for j0 in range(0, NFC, 11):
                            p.dma("sp", lambda e, op_=op_, wdt=wdt, j0=j0: e.dma_start(
                                out=wdt[:, j0:j0 + 11, :], in_=b_down[l][j0 * 128:(j0 + 11) * 128, op_ * 256:(op_ + 1) * 256].rearrange("(j p) o -> p j o", p=128)),
                                reads=["b_down%d" % l], writes=["wd_%d" % kb])
                        for oi in range(2):
                            o = op_ * 2 + oi
                            bank = 1 + (o % 2)
                            for f in range(NFC):
                                p.op("pe", lambda e, f=f, oi=oi, bank=bank, wdt=wdt: e.matmul(ps[bank][:, 0:W], wdt[:, f, oi * 128:(oi + 1) * 128], hh[:, f, :],
                                                                                               start=(f == 0), stop=(f == NFC - 1)),
                                     reads=["wd_%d" % kb, "hh"], writes=[psn[bank]])
                            p.op("dve", lambda e, o=o, bank=bank, x32=x32: e.tensor_tensor(out=x32[:, o, :], in0=x32[:, o, :], in1=ps[bank][:, 0:W], op=ALU.add),
                                 reads=[xn, psn[bank]], writes=[xn])
                    if not last:
                        p.dma("sp", lambda e, t0=t0, x32=x32: e.dma_start(out=dst.rearrange("(c p) t -> p c t", p=128)[:, :, t0:t0 + W], in_=x32[:]),
                              reads=[xn], writes=["xs_next"])
                    else:
                        for b in range(4):
                            for c2 in range(2):
                                bank = 1 + ((b * 2 + c2) % 2)
                                for cq in range(4):
                                    c = c2 * 4 + cq
                                    p.op("pe", lambda e, b=b, c=c, cq=cq, bank=bank, x32=x32: e.transpose(ps[bank][:, cq * 128:(cq + 1) * 128],
                                                                                                         x32[:, c, b * 128:(b + 1) * 128], ident[:]),
                                         reads=[xn, "cf_ident"], writes=[psn[bank]])
                                p.op("act", lambda e, b=b, c2=c2, bank=bank: e.activation(out=ytile[:, b, c2 * 512:(c2 + 1) * 512], in_=ps[bank][:, 0:512], func=AF.Copy),
                                     reads=[psn[bank]], writes=["ytile"])
                        p.dma("sp", lambda e, t0=t0: e.dma_start(out=y_out[t0:t0 + W, :].rearrange("(b p) f -> p b f", p=128), in_=ytile[:]),
                              reads=["ytile"], writes=["y"])
                p.barrier()
                p.emit()

        cur = 0
        for l in range(NL):
            src, dst = xs[cur], xs[1 - cur]
            if l % 2 == 0:
                phase_E(l, src, first=(l == 0))
            else:
                phase_A(l, src)
                phase_B(l)
            phase_F(l, src, dst, last=(l == NL - 1))
            cur = 1 - cur
        nc._mk_ninst = p.ninst
    return nc


def _host_layout(inp, NL):
    NE = (NL + 1) // 2
    f = lambda a: np.ascontiguousarray(np.asarray(a, dtype=np.float32))
    norms = np.zeros((128, NL * 16), np.float32)
    for l in range(NL):
        i = l // 2
        g1 = f(inp["even_norm"])[i] if l % 2 == 0 else f(inp["odd_norm"])[i]
        g2 = f(inp["ffn_norm"])[l]
        norms[:, (2 * l) * 8:(2 * l) * 8 + 8] = g1.reshape(8, 128).T
        norms[:, (2 * l + 1) * 8:(2 * l + 1) * 8 + 8] = g2.reshape(8, 128).T
    ne = max(NE, 1)
    qkg = np.zeros((64, ne * 2), np.float32)
    sinks = np.zeros((1, ne * 8), np.float32)
    ssmv = np.zeros((128, ne * 48), np.float32)
    ssmb = np.zeros((128, ne * 512), np.float32)
    ssmc = np.zeros((128, ne * 4096), np.float32)
    ssmd = np.zeros((128, ne * 8), np.float32)
    for i in range(NE):
        qkg[:, 2 * i] = f(inp["q_norm"])[i]
        qkg[:, 2 * i + 1] = f(inp["k_norm"])[i]
        sinks[0, 8 * i:8 * i + 8] = f(inp["sinks"])[i]
        ssmv[:, 48 * i:48 * i + 16] = f(inp["ssm_a_re"])[i].reshape(16, 128).T
        ssmv[:, 48 * i + 16:48 * i + 32] = f(inp["ssm_a_im"])[i].reshape(16, 128).T
        ssmv[:, 48 * i + 32:48 * i + 48] = np.repeat(f(inp["ssm_log_dt"])[i], 64).reshape(16, 128).T
        for ri, nm in enumerate(["ssm_b_re", "ssm_b_im"]):
            b = f(inp[nm])[i].reshape(16, 128, 16).transpose(1, 0, 2)
            ssmb[:, 512 * i + ri * 256:512 * i + (ri + 1) * 256] = b.reshape(128, 256)
        for ri, nm in enumerate(["ssm_c_re", "ssm_c_im"]):
            c = f(inp[nm])[i]
            pad = np.zeros((16, 128, 128), np.float32)
            for st in range(16):
                for hh in range(2):
                    g = 2 * st + hh
                    col = (g % 8) * 16
                    pad[st, hh * 64:(hh + 1) * 64, col:col + 16] = c[g].T
            ssmc[:, 4096 * i + ri * 2048:4096 * i + (ri + 1) * 2048] = pad.transpose(1, 0, 2).reshape(128, 2048)
        ssmd[:, 8 * i:8 * i + 4] = f(inp["ssm_d"])[i].reshape(4, 128).T
        ssmd[:, 8 * i + 4:8 * i + 8] = f(inp["ssm_b_glu"])[i].reshape(4, 128).T
    return dict(norms=norms, qkg=qkg, sinks=sinks, ssmv=ssmv, ssmb=ssmb, ssmc=ssmc, ssmd=ssmd, consts=make_consts())


_WNAMES = ["even_w_in", "ssm_w_glu", "even_w_out", "odd_w_in", "odd_w_out", "ffn_w_gate", "ffn_w_up", "ffn_w_down"]


def run_trunk(inp, SEQ, NL, ncores, trace=False):
    nc = build_program(SEQ, NL)
    shared = _host_layout(inp, NL)
    NE, NO = (NL + 1) // 2, NL // 2
    for nm in _WNAMES:
        a = np.ascontiguousarray(np.asarray(inp[nm], dtype=np.float32))
        n = NL if nm.startswith("ffn") else (NE if (nm.startswith("even") or nm.startswith("ssm")) else NO)
        shared[nm] = np.ascontiguousarray(a[:max(n, 1)])
    x = np.asarray(inp["x"], dtype=np.float32)
    in_maps = []
    for b in range(ncores):
        m = dict(shared)
        m["x"] = np.ascontiguousarray(x[b, :SEQ])
        in_maps.append(m)
    res = run_bass_kernel_spmd(nc, in_maps, core_ids=list(range(ncores)), trace=trace)
    out = np.stack([np.asarray(r["y"], dtype=np.float32) for r in res.results], axis=0)
    return out, res


def kernel(**inputs):
    out, _ = run_trunk(inputs, 4096, 4, 8)
    return out
```
